# Optimizing a Trainium2 kernel written in Bass

```python
import math
import jax, jax.numpy as jnp
from jax import lax
import numpy as np

D_MODEL = 2048
BATCH = 4
SEQ = 4096
DEPTH = 2

HEAD_DIM = 128
CHUNK = 128
QB = 128
N_A_GROUPS = 8
A_WIDTH = N_A_GROUPS * HEAD_DIM
N_B_HEADS = 8
DILATED_CONFIGS = ((128, 1), (512, 4), (2048, 16))
N_B_GROUPS = len(DILATED_CONFIGS)
B_WIDTH = N_B_HEADS * HEAD_DIM
MIX_WIDTH = A_WIDTH + B_WIDTH
IN_WIDTH = 2 * A_WIDTH + N_B_GROUPS * B_WIDTH + 2 * B_WIDTH
D_FF = -(-8 * D_MODEL // (3 * 256)) * 256
ROPE_THETA = 10000.0
EPS = 1e-6

kernel_name = "hybrid_gmlp_dilated_attn_block"


def rms_norm(x, g):
    x32 = x.astype(jnp.float32)
    y = x32 * lax.rsqrt(jnp.mean(x32 * x32, axis=-1, keepdims=True) + EPS)
    return (y * g.astype(jnp.float32)).astype(x.dtype)


def layer_norm(x, g, b):
    x32 = x.astype(jnp.float32)
    mu = jnp.mean(x32, axis=-1, keepdims=True)
    var = jnp.mean(jnp.square(x32 - mu), axis=-1, keepdims=True)
    y = (x32 - mu) * lax.rsqrt(var + EPS)
    return (y * g.astype(jnp.float32) + b.astype(jnp.float32)).astype(x.dtype)


def rope_tables(seq):
    pos = jnp.arange(seq, dtype=jnp.float32)
    inv_freq = 1.0 / (ROPE_THETA ** (jnp.arange(0, HEAD_DIM, 2, dtype=jnp.float32) / HEAD_DIM))
    ang = pos[:, None] * inv_freq[None, :]
    return jnp.cos(ang), jnp.sin(ang)


def apply_rope(x, cos, sin):
    half = HEAD_DIM // 2
    x32 = x.astype(jnp.float32)
    x1, x2 = x32[..., :half], x32[..., half:]
    c = cos[None, :, None, :]
    s = sin[None, :, None, :]
    return jnp.concatenate([x1 * c - x2 * s, x2 * c + x1 * s], axis=-1).astype(x.dtype)


def chunked_spatial_gating(z, ln_g, ln_b, w_s, b_s):
    B, S, _ = z.shape
    z = jax.nn.gelu(z, approximate=False)
    u, v = jnp.split(z, 2, axis=-1)
    v = layer_norm(v.reshape(B, S, N_A_GROUPS, HEAD_DIM), ln_g, ln_b)
    v = v.reshape(B, S // CHUNK, CHUNK, N_A_GROUPS, HEAD_DIM)
    causal = jnp.tril(jnp.ones((CHUNK, CHUNK), dtype=w_s.dtype))
    w = w_s * causal[None]
    gate = jnp.einsum('gij,bnjgc->bnigc', w, v) + b_s.T[None, None, :, :, None]
    return u * gate.reshape(B, S, A_WIDTH).astype(u.dtype)


def _to_residue_blocks(x, r, pad):
    B, S = x.shape[:2]
    rest = x.shape[2:]
    x = jnp.pad(x, ((0, 0), (0, pad)) + ((0, 0),) * len(rest))
    L = (S + pad) // r
    x = jnp.moveaxis(x.reshape((B, L, r) + rest), 2, 1)
    return x.reshape((B, r, L // QB, QB) + rest)


def _from_residue_blocks(x, S):
    B, r, nb = x.shape[:3]
    rest = x.shape[4:]
    x = jnp.moveaxis(x.reshape((B, r, nb * QB) + rest), 1, 2)
    return x.reshape((B, nb * QB * r) + rest)[:, :S]


def dilated_branch(q, k, v, window, dilation):
    B, S, H, D = q.shape
    r = dilation
    w_sub = window // dilation
    pad = (-S) % (r * QB)
    qb = _to_residue_blocks(q, r, pad).astype(jnp.float32)
    kb = _to_residue_blocks(k, r, pad).astype(jnp.float32)
    vb = _to_residue_blocks(v, r, pad).astype(jnp.float32)
    nb = qb.shape[2]
    blk_pad = ((0, 0), (0, 0), (1, 0), (0, 0), (0, 0), (0, 0))
    kk = jnp.concatenate([jnp.pad(kb, blk_pad)[:, :, :-1], kb], axis=3)
    vv = jnp.concatenate([jnp.pad(vb, blk_pad)[:, :, :-1], vb], axis=3)
    s = jnp.einsum('brnqhd,brnkhd->brnhqk', qb, kk) * (D ** -0.5)
    qi = jnp.arange(QB)[:, None]
    kj = jnp.arange(2 * QB)[None, :]
    band = (kj <= QB + qi) & (kj >= QB + qi - w_sub)
    valid = (jnp.arange(nb) > 0)[:, None, None] | (kj >= QB)[None]
    mask = band[None] & valid
    s = jnp.where(mask[None, None, :, None], s, -jnp.inf)
    m = jnp.max(s, axis=-1, keepdims=True)
    p = jnp.exp(s - m)
    l = jnp.sum(p, axis=-1, keepdims=True)
    o = jnp.einsum('brnhqk,brnkhd->brnqhd', p, vv) / jnp.transpose(l, (0, 1, 2, 4, 3, 5))
    lse = jnp.transpose((m + jnp.log(l))[..., 0], (0, 1, 2, 4, 3))
    return _from_residue_blocks(o, S), _from_residue_blocks(lse, S)


def dilated_attention(q, k, v, cos, sin, q_gain, k_gain):
    B, S = q.shape[:2]
    q = q.reshape(B, S, N_B_GROUPS * N_B_HEADS, HEAD_DIM)
    k = k.reshape(B, S, N_B_HEADS, HEAD_DIM)
    v = v.reshape(B, S, N_B_HEADS, HEAD_DIM)
    q = apply_rope(rms_norm(q, q_gain), cos, sin).reshape(B, S, N_B_GROUPS, N_B_HEADS, HEAD_DIM)
    k = apply_rope(rms_norm(k, k_gain), cos, sin)
    outs, lses = [], []
    for gi, (window, dilation) in enumerate(DILATED_CONFIGS):
        o, lse = dilated_branch(q[:, :, gi], k, v, window, dilation)
        outs.append(o)
        lses.append(lse)
    alpha = jax.nn.softmax(jnp.stack(lses, axis=0), axis=0)
    o = jnp.sum(alpha[..., None] * jnp.stack(outs, axis=0), axis=0)
    return o.reshape(B, S, B_WIDTH).astype(v.dtype)


def setup_inputs(seed: int = 0) -> dict:
    key = jax.random.key(seed)
    ks = jax.random.split(key, 17)
    f32 = jnp.float32
    nrm = lambda k, shape, scale: jax.random.normal(k, shape, f32) * scale
    gain = lambda k, shape: 1.0 + 0.02 * jax.random.normal(k, shape, f32)
    return {
        "x": jax.random.normal(ks[0], (BATCH, SEQ, D_MODEL), f32),
        "mix_norm": gain(ks[1], (DEPTH, D_MODEL)),
        "w_in": nrm(ks[2], (DEPTH, D_MODEL, IN_WIDTH), D_MODEL ** -0.5),
        "a_ln_g": gain(ks[3], (DEPTH, N_A_GROUPS, HEAD_DIM)),
        "a_ln_b": nrm(ks[4], (DEPTH, N_A_GROUPS, HEAD_DIM), 0.02),
        "a_w_s": nrm(ks[5], (DEPTH, N_A_GROUPS, CHUNK, CHUNK), CHUNK ** -0.5),
        "a_b_s": gain(ks[6], (DEPTH, N_A_GROUPS, CHUNK)),
        "q_norm": gain(ks[7], (DEPTH, HEAD_DIM)),
        "k_norm": gain(ks[8], (DEPTH, HEAD_DIM)),
        "a_out_norm": gain(ks[9], (DEPTH, A_WIDTH)),
        "b_out_norm": gain(ks[10], (DEPTH, B_WIDTH)),
        "w_out": nrm(ks[11], (DEPTH, MIX_WIDTH, D_MODEL), MIX_WIDTH ** -0.5),
        "ffn_norm": gain(ks[12], (DEPTH, D_MODEL)),
        "w_gate": nrm(ks[13], (DEPTH, D_MODEL, D_FF), D_MODEL ** -0.5),
        "w_up": nrm(ks[14], (DEPTH, D_MODEL, D_FF), D_MODEL ** -0.5),
        "w_down": nrm(ks[15], (DEPTH, D_FF, D_MODEL), D_FF ** -0.5),
    }


def reference(x, mix_norm, w_in, a_ln_g, a_ln_b, a_w_s, a_b_s, q_norm, k_norm,
              a_out_norm, b_out_norm, w_out, ffn_norm, w_gate, w_up, w_down):
    B, S, _ = x.shape
    cos, sin = rope_tables(S)
    c_a = 2 * A_WIDTH
    c_q = c_a + N_B_GROUPS * B_WIDTH
    c_k = c_q + B_WIDTH
    for l in range(DEPTH):
        h = rms_norm(x, mix_norm[l])
        z = h @ w_in[l]
        a_out = chunked_spatial_gating(z[..., :c_a], a_ln_g[l], a_ln_b[l], a_w_s[l], a_b_s[l])
        b_out = dilated_attention(z[..., c_a:c_q], z[..., c_q:c_k], z[..., c_k:],
                                  cos, sin, q_norm[l], k_norm[l])
        mixed = jnp.concatenate([rms_norm(a_out, a_out_norm[l]),
                                 rms_norm(b_out, b_out_norm[l])], axis=-1)
        x = x + mixed @ w_out[l]
        h = rms_norm(x, ffn_norm[l])
        x = x + (jax.nn.silu(h @ w_gate[l]) * (h @ w_up[l])) @ w_down[l]
    return x
```

```python
import numpy as np
from contextlib import ExitStack
import concourse.bass as bass
import concourse.mybir as mybir
from concourse.bass_utils import run_bass_kernel_spmd

F32 = mybir.dt.float32
BF16 = mybir.dt.bfloat16
AF = mybir.ActivationFunctionType
ALU = mybir.AluOpType
AX = mybir.AxisListType

P = 128
D = 2048
T = 2048
NT = T // P
KC = D // P
INW = 7168
DFF = 5632
FC = DFF // P
EPS = 1e-6
NEG = -30000.0
SM_SCALE = 128.0 ** -0.5
SM_BIAS = -8.0
DIL = (1, 4, 16)


class Buf:
    __slots__ = ("name", "lw", "rd")

    def __init__(self, name=""):
        self.name = name
        self.lw = None
        self.rd = {}


class FW:
    ENG = ("pe", "act", "dve", "pool", "sp")

    def __init__(self, nc, n_dma_sems=40):
        self.nc = nc
        self.prog = {e: [] for e in self.ENG}
        self.sem = {e: nc.alloc_semaphore("s_" + e) for e in self.ENG}
        self.cnt = {e: 0 for e in self.ENG}
        self.seen = {e: {} for e in self.ENG}
        self.dsem = [nc.alloc_semaphore(f"d{i}") for i in range(n_dma_sems)]
        self.dcnt = [0] * n_dma_sems
        self.dnext = 0

    def _wait(self, e, tok):
        key, val = tok
        if key == e and e == "pe":
            return
        if self.seen[e].get(key, 0) >= val:
            return
        self.seen[e][key] = val
        sem = self.sem[key] if isinstance(key, str) else self.dsem[key]
        self.prog[e].append(("w", sem, val))

    def _deps(self, e, reads, writes):
        for r in reads:
            if r.lw is not None:
                self._wait(e, r.lw)
        for w in writes:
            if w.lw is not None:
                self._wait(e, w.lw)
            for t in w.rd.values():
                self._wait(e, t)

    @staticmethod
    def _mark(tok, reads, writes):
        for r in reads:
            r.rd[tok[0]] = tok
        for w in writes:
            w.lw = tok
            w.rd = {}

    def op(self, e, fn, reads=(), writes=(), inc=True):
        self._deps(e, reads, writes)
        if inc:
            self.cnt[e] += 1
            tok = (e, self.cnt[e])
            self.prog[e].append(("i", fn, self.sem[e]))
        else:
            tok = (e, self.cnt[e] + 1)
            self.prog[e].append(("i", fn, None))
        self._mark(tok, reads, writes)
        return tok

    def dma(self, q, fn, reads=(), writes=()):
        i = self.dnext
        self.dnext = (self.dnext + 1) % len(self.dsem)
        if self.dcnt[i] > 0:
            self._wait(q, (i, self.dcnt[i]))
        self._deps(q, reads, writes)
        self.dcnt[i] += 16
        tok = (i, self.dcnt[i])
        self.prog[q].append(("d", fn, self.dsem[i]))
        self._mark(tok, reads, writes)
        return tok

    def barrier(self):
        for e in self.ENG:
            for e2 in self.ENG:
                if e2 != e and self.cnt[e2] > 0:
                    self._wait(e, (e2, self.cnt[e2]))
            for i, c in enumerate(self.dcnt):
                if c > 0:
                    self._wait(e, (i, c))

    def emit(self):
        nc = self.nc
        prog = self.prog

        def run(eng, lst):
            for it in lst:
                if it[0] == "w":
                    eng.wait_ge(it[1], it[2])
                elif it[0] == "i":
                    ins = it[1](eng)
                    if it[2] is not None:
                        ins.then_inc(it[2], 1)
                else:
                    it[1](eng).then_inc(it[2], 16)

        with nc.Block() as block:
            @block.tensor
            def _(e):
                run(e, prog["pe"])

            @block.scalar
            def _(e):
                run(e, prog["act"])

            @block.vector
            def _(e):
                run(e, prog["dve"])

            @block.gpsimd
            def _(e):
                run(e, prog["pool"])

            @block.sync
            def _(e):
                run(e, prog["sp"])


def V(name, *a, **k):
    return lambda e: getattr(e, name)(*a, **k)


def rsqrt(f, out, in_, scale, b):
    f.op("act", V("activation", out=out, in_=in_, func=AF.Sqrt, bias=EPS, scale=scale), writes=[b])
    f.op("dve", V("reciprocal", out=out, in_=out), writes=[b])


class Rot:
    def __init__(self, items):
        self.items = items
        self.i = 0

    def next(self):
        it = self.items[self.i % len(self.items)]
        self.i += 1
        return it


def emit_layer(nc, f, es0, io, L, dbg=None):
    sb = lambda es, name, shape, dt: es.enter_context(nc.sbuf_tensor(f"{name}_{L}", shape, dt))
    ps = lambda es, name, shape, dt: es.enter_context(nc.psum_tensor(f"{name}_{L}", shape, dt))

    es_c = ExitStack()
    gcol = sb(es_c, "gcol", [P, 48], F32); b_gcol = Buf()
    bscol = sb(es_c, "bscol", [P, 8], F32); b_bscol = Buf()
    identf = sb(es_c, "identf", [P, P], F32); b_identf = Buf()
    identb = sb(es_c, "identb", [P, P], BF16); b_identb = Buf()
    onesb = sb(es_c, "onesb", [P, P], BF16); b_onesb = Buf()
    nbias = sb(es_c, "nbias", [P, 1], F32); b_nbias = Buf()
    mbf = sb(es_c, "mbf", [P, 2, 2, P], F32); b_mbf = Buf()
    mbb = sb(es_c, "mbb", [P, 2, 2, P], BF16); b_mbb = Buf()
    f.dma("sp", V("dma_start", out=gcol[:], in_=io["gcol"]), writes=[b_gcol])
    f.dma("sp", V("dma_start", out=bscol[:], in_=io["bs_col"]), writes=[b_bscol])
    f.dma("sp", V("dma_start", out=identf[:], in_=io["ident"]), writes=[b_identf])
    f.dma("sp", V("dma_start", out=mbf[:], in_=io["mb"]), writes=[b_mbf])
    f.op("dve", V("tensor_copy", out=identb[:], in_=identf[:]), reads=[b_identf], writes=[b_identb])
    f.op("dve", V("tensor_copy", out=mbb[:], in_=mbf[:]), reads=[b_mbf], writes=[b_mbb])
    f.op("dve", V("memset", onesb[:], 1.0), writes=[b_onesb])
    f.op("dve", V("memset", nbias[:], SM_BIAS), writes=[b_nbias])

    aT = sb(es_c, "aT", [P, 8, T], BF16); b_aT = Buf()
    ssa = sb(es_c, "ssa", [P, NT, 4], F32); b_ssa = Buf()
    ssb = sb(es_c, "ssb", [P, 8, NT], F32); b_ssb = Buf()
    rstd_ab = sb(es_c, "rstd_ab", [P, 2, NT], F32); b_rstd_ab = Buf()

    es1 = ExitStack()
    rowp = sb(es1, "rowp", [P, 2304], F32); b_rowp = Buf()
    wsT = sb(es1, "wsT", [P, 8, P], F32); b_wsT = Buf()
    trilf = sb(es1, "trilf", [P, P], F32); b_tril = Buf()
    wsm = sb(es1, "wsm", [P, 8, P], BF16); b_wsm = Buf()
    rs = sb(es1, "rs", [P, 8], F32); b_rs = Buf()
    Cg = sb(es1, "Cg", [P, 8, P], F32); b_Cg = Buf()
    cs = sb(es1, "cs", [P, NT, 2, 64], F32); b_cs = Buf()
    f.dma("sp", V("dma_start", out=rowp[:], in_=io["rowp"]), writes=[b_rowp])
    f.dma("sp", V("dma_start", out=wsT[:], in_=io["wsT"]), writes=[b_wsT])
    f.dma("sp", V("dma_start", out=trilf[:], in_=io["tril"]), writes=[b_tril])
    lng = rowp[:, 0:1024].rearrange("p (g c) -> p g c", g=8)
    lnb = rowp[:, 1024:2048].rearrange("p (g c) -> p g c", g=8)
    qg = rowp[:, 2048:2176]
    kg = rowp[:, 2176:2304]

    hT = sb(es1, "hT", [P, KC, T], BF16); b_hT = Buf()
    xin = [(sb(es1, f"xin{i}", [P, D], F32), Buf()) for i in range(1)]
    xn = sb(es1, "xn", [P, D], BF16); b_xn = Buf()
    ss1 = sb(es1, "ss1", [P, 2], F32); b_ss1 = Buf()
    wp = [(sb(es1, f"wp{i}", [P, KC, 512], BF16), Buf()) for i in range(2)]
    stg = sb(es1, "stg", [P, 4, T], BF16); b_stg = Buf()
    ge = sb(es1, "ge", [P, 4, P], F32); b_ge = Buf()
    sq = sb(es1, "sq", [P, 4, P], F32); b_sq = Buf()
    st8 = sb(es1, "st8", [P, 8, 4], F32); b_st8 = Buf()
    vn = sb(es1, "vn", [P, 2, P], BF16); b_vn = Buf()
    tg = sb(es1, "tg", [P, 2, P], F32); b_tg = Buf()
    ab = sb(es1, "ab", [P, 2, P], BF16); b_ab = Buf()
    zg = sb(es1, "zg", [P, 4, 2, 64], F32); b_zg = Buf()
    tr = sb(es1, "tr", [P, 4, 4, 64], F32); b_tr = Buf()
    ro = sb(es1, "ro", [P, 4, 2, 64], BF16); b_ro = Buf()
    junk = sb(es1, "junk", [P, 2, P], BF16); b_junk = Buf()

    pT = ps(es1, "pT", [P, KC, P], BF16); b_pT = Buf()
    acc = [(ps(es1, f"acc{i}", [P, 4, P], F32), Buf()) for i in range(2)]
    pg = ps(es1, "pg", [P, 4, P], F32); b_pg = Buf()
    pTs = ps(es1, "pTs", [P, 8, P], BF16); b_pTs = Buf()
    prs = ps(es1, "prs", [P, 512], F32); b_prs = Buf()

    f.op("dve", V("tensor_tensor", out=wsm[:], in0=wsT[:], in1=trilf[:].unsqueeze(1).to_broadcast([P, 8, P]),
                  op=ALU.mult), reads=[b_wsT, b_tril], writes=[b_wsm])
    for g in range(8):
        f.op("pe", V("matmul", prs[:, g:g + 1], lhsT=wsm[:, g, :], rhs=onesb[:, 0:1], start=True, stop=True),
             reads=[b_wsm, b_onesb], writes=[b_prs])
    f.op("dve", V("tensor_copy", out=rs[:], in_=prs[:, 0:8]), reads=[], writes=[b_rs, b_prs])
    for g in range(8):
        f.op("dve", V("tensor_scalar", out=Cg[:, g, :], in0=lnb[:, g, :], scalar1=rs[:, g:g + 1],
                      scalar2=bscol[:, g:g + 1], op0=ALU.mult, op1=ALU.add),
             reads=[b_rowp, b_rs, b_bscol], writes=[b_Cg])

    def build_hT(src, goff, hT_t, b_hT_t, ntiles, xin_l, xn_t, b_xn_t, ss_t, b_ss_t, pT_t, b_pT_t):
        for t in range(ntiles):
            xt, b_xt = xin_l[t % len(xin_l)]
            f.dma("sp", V("dma_start", out=xt[:], in_=src[t * P:(t + 1) * P, :]), writes=[b_xt])
            f.op("act", V("activation", out=xn_t[:], in_=xt[:], func=AF.Square, accum_out=ss_t[:, 0:1]),
                 reads=[b_xt], writes=[b_xn_t, b_ss_t])
            rsqrt(f, ss_t[:, 1:2], ss_t[:, 0:1], 1.0 / D, b_ss_t)
            f.op("act", V("activation", out=xn_t[:], in_=xt[:], func=AF.Copy, scale=ss_t[:, 1:2]),
                 reads=[b_xt, b_ss_t], writes=[b_xn_t])
            for c in range(KC):
                f.op("pe", V("transpose", out=pT_t[:, c, :], in_=xn_t[:, c * P:(c + 1) * P], identity=identb[:]),
                     reads=[b_xn_t, b_identb], writes=[b_pT_t], inc=(c == KC - 1))
            f.op("dve", V("tensor_tensor", out=hT_t[:, :, t * P:(t + 1) * P], in0=pT_t[:],
                          in1=gcol[:, goff:goff + KC].unsqueeze(2).to_broadcast([P, KC, P]), op=ALU.mult),
                 reads=[b_gcol], writes=[b_hT_t, b_pT_t])

    w_in = io["w_in"]

    def load_panel(wt, b_wt, colspecs):
        off = 0
        for (c0, n) in colspecs:
            f.dma("pool", V("dma_start", out=wt[:, :, off:off + n],
                            in_=w_in[:, c0:c0 + n].rearrange("(c p) n -> p c n", p=P)), writes=[b_wt])
            off += n

    C_A = 2048
    C_Q = C_A + 3072
    C_K = C_Q + 1024

    def panel_list(own):
        pl = []
        if own:
            for pi in range(4):
                g0 = 2 * pi
                pl.append(("g", [(g0 * P, 256), (1024 + g0 * P, 256)], pi))
            for g in range(3):
                for hh in range(2):
                    pl.append(("q", [(C_A + g * 1024 + hh * 512, 512)], (g, hh)))
        for hh in range(2):
            pl.append(("k", [(C_Q + hh * 512, 512)], hh))
        for hh in range(2):
            pl.append(("v", [(C_K + hh * 512, 512)], hh))
        return pl

    wp_rot = Rot(wp)
    acc_rot = Rot(acc)

    def post_g(pacc, b_pacc, t, pi):
        g0 = 2 * pi
        tsl = slice(t * P, (t + 1) * P)
        f.op("act", V("activation", out=ge[:], in_=pacc[:], func=AF.Gelu), reads=[], writes=[b_ge, b_pacc])
        v = ge[:, 2:4, :]
        f.op("dve", V("reduce_sum", out=st8[:, 0, 0:2], in_=v, axis=AX.X), reads=[b_ge], writes=[b_st8])
        f.op("act", V("activation", out=sq[:, 0:2, :], in_=v, func=AF.Square), reads=[b_ge], writes=[b_sq])
        f.op("dve", V("reduce_sum", out=st8[:, 1, 0:2], in_=sq[:, 0:2, :], axis=AX.X), reads=[b_sq], writes=[b_st8])
        f.op("dve", V("tensor_scalar", out=st8[:, 2, 0:2], in0=st8[:, 0, 0:2], scalar1=1.0 / P, scalar2=None,
                      op0=ALU.mult), writes=[b_st8])
        f.op("dve", V("tensor_tensor", out=st8[:, 3, 0:2], in0=st8[:, 2, 0:2], in1=st8[:, 2, 0:2], op=ALU.mult),
             writes=[b_st8])
        f.op("dve", V("scalar_tensor_tensor", out=st8[:, 4, 0:2], in0=st8[:, 1, 0:2], scalar=1.0 / P,
                      in1=st8[:, 3, 0:2], op0=ALU.mult, op1=ALU.subtract), writes=[b_st8])
        rsqrt(f, st8[:, 5, 0:2], st8[:, 4, 0:2], 1.0, b_st8)
        for gi in range(2):
            f.op("dve", V("tensor_scalar", out=vn[:, gi, :], in0=ge[:, 2 + gi, :], scalar1=st8[:, 2, gi:gi + 1],
                          scalar2=st8[:, 5, gi:gi + 1], op0=ALU.subtract, op1=ALU.mult),
                 reads=[b_ge, b_st8], writes=[b_vn])
        for gi in range(2):
            f.op("pe", V("matmul", pg[:, gi, :], lhsT=wsm[:, g0 + gi, :], rhs=vn[:, gi, :], start=True, stop=True),
                 reads=[b_wsm, b_vn], writes=[b_pg], inc=(gi == 1))
        f.op("dve", V("tensor_tensor", out=tg[:], in0=pg[:, 0:2, :], in1=lng[:, g0:g0 + 2, :], op=ALU.mult),
             reads=[b_rowp], writes=[b_tg, b_pg])
        f.op("dve", V("tensor_tensor", out=tg[:], in0=tg[:], in1=Cg[:, g0:g0 + 2, :], op=ALU.add),
             reads=[b_Cg], writes=[b_tg])
        f.op("dve", V("tensor_tensor", out=ab[:], in0=tg[:], in1=ge[:, 0:2, :], op=ALU.mult),
             reads=[b_tg, b_ge], writes=[b_ab])
        f.op("act", V("activation", out=junk[:], in_=ab[:], func=AF.Square, accum_out=ssa[:, t, pi:pi + 1]),
             reads=[b_ab], writes=[b_junk, b_ssa])
        for gi in range(2):
            f.op("pe", V("transpose", out=pTs[:, gi, :], in_=ab[:, gi, :], identity=identb[:]),
                 reads=[b_ab, b_identb], writes=[b_pTs], inc=(gi == 1))
        f.op("dve", V("tensor_tensor", out=aT[:, g0:g0 + 2, tsl], in0=pTs[:, 0:2, :],
                      in1=gcol[:, 32 + g0:32 + g0 + 2].unsqueeze(2).to_broadcast([P, 2, P]), op=ALU.mult),
             reads=[b_gcol], writes=[b_aT, b_pTs])

    def post_qk(pacc, b_pacc, t, gain):
        tsl = slice(t * P, (t + 1) * P)
        f.op("act", V("activation", out=sq[:], in_=pacc[:], func=AF.Square), reads=[], writes=[b_sq, b_pacc])
        f.op("dve", V("reduce_sum", out=st8[:, 6, :], in_=sq[:], axis=AX.X), reads=[b_sq], writes=[b_st8])
        rsqrt(f, st8[:, 7, :], st8[:, 6, :], 1.0 / P, b_st8)
        zgf = zg[:].rearrange("p h a d -> p h (a d)")
        for hh in range(4):
            f.op("dve", V("scalar_tensor_tensor", out=zgf[:, hh, :], in0=pacc[:, hh, :], scalar=st8[:, 7, hh:hh + 1],
                          in1=gain, op0=ALU.mult, op1=ALU.mult),
                 reads=[b_st8, b_rowp], writes=[b_zg, b_pacc])
        cosb = cs[:, t, 0, :].unsqueeze(1).to_broadcast([P, 4, 64])
        sinb = cs[:, t, 1, :].unsqueeze(1).to_broadcast([P, 4, 64])
        f.op("dve", V("tensor_tensor", out=tr[:, :, 0, :], in0=zg[:, :, 0, :], in1=cosb, op=ALU.mult),
             reads=[b_zg, b_cs], writes=[b_tr])
        f.op("dve", V("tensor_tensor", out=tr[:, :, 1, :], in0=zg[:, :, 1, :], in1=sinb, op=ALU.mult),
             reads=[b_zg, b_cs], writes=[b_tr])
        f.op("dve", V("tensor_tensor", out=tr[:, :, 2, :], in0=zg[:, :, 1, :], in1=cosb, op=ALU.mult),
             reads=[b_zg, b_cs], writes=[b_tr])
        f.op("dve", V("tensor_tensor", out=tr[:, :, 3, :], in0=zg[:, :, 0, :], in1=sinb, op=ALU.mult),
             reads=[b_zg, b_cs], writes=[b_tr])
        f.op("dve", V("tensor_tensor", out=ro[:, :, 0, :], in0=tr[:, :, 0, :], in1=tr[:, :, 1, :], op=ALU.subtract),
             reads=[b_tr], writes=[b_ro])
        f.op("dve", V("tensor_tensor", out=ro[:, :, 1, :], in0=tr[:, :, 2, :], in1=tr[:, :, 3, :], op=ALU.add),
             reads=[b_tr], writes=[b_ro])
        rof = ro[:].rearrange("p h a d -> p h (a d)")
        for hh in range(4):
            f.op("pe", V("transpose", out=pTs[:, hh, :], in_=rof[:, hh, :], identity=identb[:]),
                 reads=[b_ro, b_identb], writes=[b_pTs], inc=(hh == 3))
        f.op("act", V("activation", out=stg[:, :, tsl], in_=pTs[:, 0:4, :], func=AF.Copy),
             reads=[], writes=[b_stg, b_pTs])

    def post_v(pacc, b_pacc, t):
        tsl = slice(t * P, (t + 1) * P)
        rof = ro[:].rearrange("p h a d -> p h (a d)")
        f.op("act", V("activation", out=rof, in_=pacc[:], func=AF.Copy), reads=[], writes=[b_ro, b_pacc])
        for hh in range(4):
            f.op("pe", V("transpose", out=pTs[:, hh, :], in_=rof[:, hh, :], identity=identb[:]),
                 reads=[b_ro, b_identb], writes=[b_pTs], inc=(hh == 3))
        f.op("act", V("activation", out=stg[:, :, tsl], in_=pTs[:, 0:4, :], func=AF.Copy),
             reads=[], writes=[b_stg, b_pTs])

    b_qT = [Buf() for _ in range(24)]
    b_kT = [Buf() for _ in range(8)]
    b_vT = [Buf() for _ in range(8)]
    qT_d, kT_d, vT_d = io["qT_d"], io["kT_d"], io["vT_d"]

    for pas in range(2):
        src = io["xp"] if pas == 0 else io["xo"]
        f.dma("sp", V("dma_start", out=cs[:], in_=io["cs"][pas]), writes=[b_cs])
        build_hT(src, 0, hT, b_hT, NT, xin, xn, b_xn, ss1, b_ss1, pT, b_pT)
        for (kind, colspecs, arg) in panel_list(pas == 1):
            wt, b_wt = wp_rot.next()
            load_panel(wt, b_wt, colspecs)
            for t in range(NT):
                pacc, b_pacc = acc_rot.next()
                paccf = pacc[:].rearrange("p a b -> p (a b)")
                for c in range(KC):
                    f.op("pe", V("matmul", paccf, lhsT=hT[:, c, t * P:(t + 1) * P], rhs=wt[:, c, :],
                                 start=(c == 0), stop=(c == KC - 1)),
                         reads=[b_hT, b_wt], writes=[b_pacc], inc=(c == KC - 1))
                if kind == "g":
                    post_g(pacc, b_pacc, t, arg)
                elif kind == "q":
                    post_qk(pacc, b_pacc, t, qg)
                elif kind == "k":
                    post_qk(pacc, b_pacc, t, kg)
                else:
                    post_v(pacc, b_pacc, t)
            if kind == "q":
                g, hh = arg
                i0 = g * 8 + hh * 4
                f.dma("sp", V("dma_start", out=qT_d[i0:i0 + 4, :, :].rearrange("h p t -> p h t"), in_=stg[:]),
                      reads=[b_stg], writes=b_qT[i0:i0 + 4])
            elif kind == "k":
                i0 = arg * 4
                f.dma("sp", V("dma_start", out=kT_d[i0:i0 + 4, :, pas * T:(pas + 1) * T].rearrange("h p t -> p h t"),
                              in_=stg[:]), reads=[b_stg], writes=b_kT[i0:i0 + 4])
            elif kind == "v":
                i0 = arg * 4
                f.dma("sp", V("dma_start", out=vT_d[i0:i0 + 4, :, pas * T:(pas + 1) * T].rearrange("h p t -> p h t"),
                              in_=stg[:]), reads=[b_stg], writes=b_vT[i0:i0 + 4])

    f.op("dve", V("reduce_sum", out=rstd_ab[:, 0, :], in_=ssa[:], axis=AX.X), reads=[b_ssa], writes=[b_rstd_ab])
    rsqrt(f, rstd_ab[:, 0, :], rstd_ab[:, 0, :], 1.0 / 1024, b_rstd_ab)
    f.barrier()
    es1.close()

    es2 = ExitStack()
    es23 = ExitStack()
    bT = sb(es23, "bT", [P, 8, T], BF16); b_bT = Buf()
    kTs = [(sb(es2, f"kTs{i}", [P, 2 * T], BF16), Buf()) for i in range(2)]
    vTs = [(sb(es2, f"vTs{i}", [P, 2 * T], BF16), Buf()) for i in range(2)]
    qTs = [(sb(es2, f"qTs{i}", [P, 3, T], BF16), Buf()) for i in range(2)]
    NVC = 17 + 20 + 32
    Vc = sb(es2, "Vc", [P, 72, P], BF16); b_Vc = Buf()
    Oacc = sb(es2, "Oacc", [P, 2, T], F32); b_Oacc = Buf()
    Pt = [(sb(es2, f"Pt{i}", [P, 2, P], BF16), Buf()) for i in range(3)]
    rec = sb(es2, "rec", [P, T], F32); b_rec = Buf()
    osq = sb(es2, "osq", [P, T], BF16); b_osq = Buf()
    pV = [(ps(es2, f"pV{i}", [P, 8, P], BF16), Buf()) for i in range(2)]
    pS = [(ps(es2, f"pS{i}", [P, 4, P], F32), Buf()) for i in range(3)]
    pO = [(ps(es2, f"pO{i}", [P, 4, P], F32), Buf()) for i in range(2)]
    pss = ps(es2, "pss", [P, 512], F32); b_pss = Buf()
    pS_rot, pO_rot, Pt_rot = Rot(pS), Rot(pO), Rot(Pt)

    def cls_view(ap2d, r, ntok):
        return ap2d.rearrange("p (n m r) -> p r n m", r=r, m=P)

    for h in range(8):
        kt, b_kt = kTs[h % 2]
        vt, b_vt = vTs[h % 2]
        qt, b_qt = qTs[h % 2]
        f.dma("sp", V("dma_start", out=kt[:], in_=kT_d[h]), reads=[b_kT[h]], writes=[b_kt])
        f.dma("sp", V("dma_start", out=vt[:], in_=vT_d[h]), reads=[b_vT[h]], writes=[b_vt])
        for g in range(3):
            f.dma("sp", V("dma_start", out=qt[:, g, :], in_=qT_d[g * 8 + h]), reads=[b_qT[g * 8 + h]], writes=[b_qt])
        vc_index = {}
        lst = []
        for g, r in enumerate(DIL):
            NB = 2 * T // (P * r)
            for c in range(r):
                for n in range(NB // 2 - 1, NB):
                    vc_index[(g, c, n)] = len(lst)
                    lst.append((r, c, n))
        assert len(lst) == NVC
        for b0 in range(0, NVC, 8):
            pv, b_pv = pV[(b0 // 8) % 2]
            nb = min(8, NVC - b0)
            for j in range(nb):
                r, c, n = lst[b0 + j]
                f.op("pe", V("transpose", out=pv[:, j, :], in_=cls_view(vt[:], r, 2 * T)[:, c, n, :], identity=identb[:]),
                     reads=[b_vt, b_identb], writes=[b_pv], inc=(j == nb - 1))
            f.op("act", V("activation", out=Vc[:, b0:b0 + nb, :], in_=pv[:, 0:nb, :], func=AF.Copy),
                 reads=[], writes=[b_Vc, b_pv])
        for g, r in enumerate(DIL):
            NB = 2 * T // (P * r)
            kv = cls_view(kt[:], r, 2 * T)
            qv = cls_view(qt[:, g, :], r, T)
            Ov = Oacc[:].rearrange("p a (n m r) -> p a r n m", r=r, m=P)
            for c in range(r):
                for n in range(NB // 2, NB):
                    nq = n - NB // 2
                    var = 1 if nq == 0 else 0
                    s_t, b_s = pS_rot.next()
                    for j, kb in enumerate((n - 1, n)):
                        f.op("pe", V("matmul", s_t[:, j, :], lhsT=kv[:, c, kb, :], rhs=qv[:, c, nq, :],
                                     start=True, stop=False), reads=[b_kt, b_qt], writes=[b_s], inc=False)
                        f.op("pe", V("matmul", s_t[:, j, :], lhsT=identb[:], rhs=mbb[:, var if j == 0 else 0, j, :],
                                     start=False, stop=True), reads=[b_identb, b_mbb], writes=[b_s], inc=(j == 1))
                    p_t, b_p = Pt_rot.next()
                    f.op("act", V("activation", out=p_t[:], in_=s_t[:, 0:2, :], func=AF.Exp, bias=nbias[:, 0:1], scale=SM_SCALE),
                         reads=[b_nbias], writes=[b_p, b_s])
                    o_t, b_o = pO_rot.next()
                    for j, kb in enumerate((n - 1, n)):
                        f.op("pe", V("matmul", o_t[:, 0, :], lhsT=Vc[:, vc_index[(g, c, kb)], :], rhs=p_t[:, j, :],
                                     start=(j == 0), stop=(j == 1)), reads=[b_Vc, b_p], writes=[b_o], inc=False)
                    for j in range(2):
                        f.op("pe", V("matmul", o_t[:, 1, :], lhsT=onesb[:], rhs=p_t[:, j, :],
                                     start=(j == 0), stop=(j == 1)), reads=[b_onesb, b_p], writes=[b_o], inc=(j == 1))
                    if g == 0:
                        f.op("dve", V("tensor_copy", out=Ov[:, :, c, nq, :], in_=o_t[:, 0:2, :]), reads=[], writes=[b_Oacc, b_o])
                    else:
                        f.op("dve", V("tensor_tensor", out=Ov[:, :, c, nq, :], in0=o_t[:, 0:2, :], in1=Ov[:, :, c, nq, :],
                                      op=ALU.add), reads=[], writes=[b_Oacc, b_o])
        f.op("dve", V("reciprocal", out=rec[:], in_=Oacc[:, 1, :]), reads=[b_Oacc], writes=[b_rec])
        f.op("dve", V("tensor_tensor", out=rec[:], in0=Oacc[:, 0, :], in1=rec[:], op=ALU.mult),
             reads=[b_Oacc], writes=[b_rec])
        f.op("act", V("activation", out=osq[:], in_=rec[:], func=AF.Square), reads=[b_rec], writes=[b_osq])
        for t in range(NT):
            f.op("pe", V("matmul", pss[:, t:t + 1], lhsT=osq[:, t * P:(t + 1) * P], rhs=onesb[:, 0:1],
                         start=True, stop=True), reads=[b_osq, b_onesb], writes=[b_pss], inc=(t == NT - 1))
        f.op("dve", V("tensor_copy", out=ssb[:, h, :], in_=pss[:, 0:NT]), reads=[], writes=[b_ssb, b_pss])
        f.op("act", V("activation", out=bT[:, h, :], in_=rec[:], func=AF.Copy, scale=gcol[:, 40 + h:41 + h]),
             reads=[b_rec, b_gcol], writes=[b_bT])
    f.op("dve", V("reduce_sum", out=rstd_ab[:, 1, :], in_=ssb[:].rearrange("p h t -> p t h"), axis=AX.X),
         reads=[b_ssb], writes=[b_rstd_ab])
    rsqrt(f, rstd_ab[:, 1, :], rstd_ab[:, 1, :], 1.0 / 1024, b_rstd_ab)
    f.barrier()
    es2.close()

    es3 = ExitStack()
    wo = [(sb(es3, f"wo{i}", [P, KC, 512], BF16), Buf()) for i in range(2)]
    xs = [(sb(es3, f"xs{i}", [P, 512], F32), Buf()) for i in range(3)]
    x1t = [(sb(es3, f"x1t{i}", [P, 512], F32), Buf()) for i in range(3)]
    pA = [(ps(es3, f"pA{i}", [P, 512], F32), Buf()) for i in range(2)]
    pB = [(ps(es3, f"pB{i}", [P, 512], F32), Buf()) for i in range(2)]
    xs_rot, x1_rot = Rot(xs), Rot(x1t)
    b_x1d = [Buf() for _ in range(NT)]
    x1_d = io["x1_d"]
    w_out = io["w_out"]
    k = 0
    for pn in range(4):
        wt, b_wt = wo[pn % 2]
        f.dma("pool", V("dma_start", out=wt[:], in_=w_out[:, pn * 512:(pn + 1) * 512].rearrange("(c p) n -> p c n", p=P)),
              writes=[b_wt])
        for t in range(NT):
            tsl = slice(t * P, (t + 1) * P)
            xt, b_xt = xs_rot.next()
            f.dma("sp", V("dma_start", out=xt[:], in_=io["xo"][tsl, pn * 512:(pn + 1) * 512]), writes=[b_xt])
            pa, b_pa = pA[k % 2]
            pb, b_pb = pB[k % 2]
            k += 1
            for c in range(8):
                f.op("pe", V("matmul", pa[:], lhsT=aT[:, c, tsl], rhs=wt[:, c, :], start=(c == 0), stop=(c == 7)),
                     reads=[b_aT, b_wt], writes=[b_pa], inc=(c == 7))
            for c in range(8):
                f.op("pe", V("matmul", pb[:], lhsT=bT[:, c, tsl], rhs=wt[:, 8 + c, :], start=(c == 0), stop=(c == 7)),
                     reads=[b_bT, b_wt], writes=[b_pb], inc=(c == 7))
            ot, b_ot = x1_rot.next()
            f.op("dve", V("scalar_tensor_tensor", out=ot[:], in0=pa[:], scalar=rstd_ab[:, 0, t:t + 1], in1=xt[:],
                          op0=ALU.mult, op1=ALU.add), reads=[b_rstd_ab, b_xt], writes=[b_ot, b_pa])
            f.op("dve", V("scalar_tensor_tensor", out=ot[:], in0=pb[:], scalar=rstd_ab[:, 1, t:t + 1], in1=ot[:],
                          op0=ALU.mult, op1=ALU.add), reads=[b_rstd_ab], writes=[b_ot, b_pb])
            f.dma("sp", V("dma_start", out=x1_d[tsl, pn * 512:(pn + 1) * 512], in_=ot[:]), reads=[b_ot],
                  writes=[b_x1d[t]])
    f.barrier()
    es3.close()
    es23.close()
    es_c.close()

    es4 = ExitStack()
    gcol4 = sb(es4, "gcol4", [P, 48], F32); b_gcol4 = Buf()
    identf4 = sb(es4, "identf4", [P, P], F32); b_if4 = Buf()
    identb4 = sb(es4, "identb4", [P, P], BF16); b_ib4 = Buf()
    f.dma("sp", V("dma_start", out=gcol4[:], in_=io["gcol"]), writes=[b_gcol4])
    f.dma("sp", V("dma_start", out=identf4[:], in_=io["ident"]), writes=[b_if4])
    f.op("dve", V("tensor_copy", out=identb4[:], in_=identf4[:]), reads=[b_if4], writes=[b_ib4])
    TB = 512
    h2T = sb(es4, "h2T", [P, KC, TB], BF16); b_h2T = Buf()
    xin4 = [(sb(es4, f"xin4{i}", [P, D], F32), Buf()) for i in range(2)]
    xn4 = sb(es4, "xn4", [P, D], BF16); b_xn4 = Buf()
    ss4 = sb(es4, "ss4", [P, 2], F32); b_ss4 = Buf()
    aff = sb(es4, "aff", [P, FC, TB], BF16); b_aff = Buf()
    wg = [(sb(es4, f"wg{i}", [P, KC, 512], BF16), Buf()) for i in range(2)]
    wu = [(sb(es4, f"wu{i}", [P, KC, 512], BF16), Buf()) for i in range(2)]
    wd = [(sb(es4, f"wd{i}", [P, FC, 256], BF16), Buf()) for i in range(2)]
    sg = [(sb(es4, f"sg{i}", [P, TB], F32), Buf()) for i in range(2)]
    xr = [(sb(es4, f"xr{i}", [P, 256], F32), Buf()) for i in range(3)]
    yo = [(sb(es4, f"yo{i}", [P, 256], F32), Buf()) for i in range(3)]
    pT4 = ps(es4, "pT4", [P, KC, P], BF16); b_pT4 = Buf()
    pG = [(ps(es4, f"pG{i}", [P, TB], F32), Buf()) for i in range(2)]
    pU = [(ps(es4, f"pU{i}", [P, TB], F32), Buf()) for i in range(2)]
    pD = [(ps(es4, f"pD{i}", [P, 512], F32), Buf()) for i in range(2)]
    w_gate, w_up, w_down = io["w_gate"], io["w_up"], io["w_down"]
    y = io["y"]
    wd_k = 0
    gu_k = 0
    sg_rot, xr_rot, yo_rot = Rot(sg), Rot(xr), Rot(yo)
    b_y = Buf()

    def build_hT4(blk):
        for tt in range(TB // P):
            t = blk * (TB // P) + tt
            xt, b_xt = xin4[tt % 2]
            f.dma("sp", V("dma_start", out=xt[:], in_=x1_d[t * P:(t + 1) * P, :]), reads=[b_x1d[t]], writes=[b_xt])
            f.op("act", V("activation", out=xn4[:], in_=xt[:], func=AF.Square, accum_out=ss4[:, 0:1]),
                 reads=[b_xt], writes=[b_xn4, b_ss4])
            rsqrt(f, ss4[:, 1:2], ss4[:, 0:1], 1.0 / D, b_ss4)
            f.op("act", V("activation", out=xn4[:], in_=xt[:], func=AF.Copy, scale=ss4[:, 1:2]),
                 reads=[b_xt, b_ss4], writes=[b_xn4])
            for c in range(KC):
                f.op("pe", V("transpose", out=pT4[:, c, :], in_=xn4[:, c * P:(c + 1) * P], identity=identb4[:]),
                     reads=[b_xn4, b_ib4], writes=[b_pT4], inc=(c == KC - 1))
            f.op("dve", V("tensor_tensor", out=h2T[:, :, tt * P:(tt + 1) * P], in0=pT4[:],
                          in1=gcol4[:, 16:32].unsqueeze(2).to_broadcast([P, KC, P]), op=ALU.mult),
                 reads=[b_gcol4], writes=[b_h2T, b_pT4])

    for blk in range(T // TB):
        build_hT4(blk)
        for jg in range(FC // 4):
            wgt, b_wgt = wg[gu_k % 2]
            wut, b_wut = wu[gu_k % 2]
            gu_k += 1
            f.dma("pool", V("dma_start", out=wgt[:], in_=w_gate[:, jg * 512:(jg + 1) * 512].rearrange("(c p) n -> p c n", p=P)),
                  writes=[b_wgt])
            f.dma("pool", V("dma_start", out=wut[:], in_=w_up[:, jg * 512:(jg + 1) * 512].rearrange("(c p) n -> p c n", p=P)),
                  writes=[b_wut])
            for jj in range(4):
                j = jg * 4 + jj
                pg_t, b_pg_t = pG[j % 2]
                pu_t, b_pu_t = pU[j % 2]
                for c in range(KC):
                    f.op("pe", V("matmul", pg_t[:], lhsT=wgt[:, c, jj * P:(jj + 1) * P], rhs=h2T[:, c, :],
                                 start=(c == 0), stop=(c == KC - 1)), reads=[b_wgt, b_h2T], writes=[b_pg_t],
                         inc=(c == KC - 1))
                for c in range(KC):
                    f.op("pe", V("matmul", pu_t[:], lhsT=wut[:, c, jj * P:(jj + 1) * P], rhs=h2T[:, c, :],
                                 start=(c == 0), stop=(c == KC - 1)), reads=[b_wut, b_h2T], writes=[b_pu_t],
                         inc=(c == KC - 1))
                s_t, b_s = sg_rot.next()
                f.op("act", V("activation", out=s_t[:], in_=pg_t[:], func=AF.Silu), reads=[], writes=[b_s, b_pg_t])
                f.op("dve", V("tensor_tensor", out=aff[:, j, :], in0=pu_t[:], in1=s_t[:], op=ALU.mult),
                     reads=[b_s], writes=[b_aff, b_pu_t])
        for pn in range(D // 256):
            wdt, b_wdt = wd[wd_k % 2]
            wd_k += 1
            f.dma("pool", V("dma_start", out=wdt[:], in_=w_down[:, pn * 256:(pn + 1) * 256].rearrange("(c p) n -> p c n", p=P)),
                  writes=[b_wdt])
            for tt in range(TB // P):
                t = blk * (TB // P) + tt
                tsl = slice(t * P, (t + 1) * P)
                xt, b_xt = xr_rot.next()
                f.dma("sp", V("dma_start", out=xt[:], in_=x1_d[tsl, pn * 256:(pn + 1) * 256]), reads=[b_x1d[t]],
                      writes=[b_xt])
                pd_t, b_pd = pD[(pn * 4 + tt) % 2]
                for j in range(FC):
                    f.op("pe", V("matmul", pd_t[:, 0:256], lhsT=aff[:, j, tt * P:(tt + 1) * P], rhs=wdt[:, j, :],
                                 start=(j == 0), stop=(j == FC - 1)), reads=[b_aff, b_wdt], writes=[b_pd],
                         inc=(j == FC - 1))
                ot, b_ot = yo_rot.next()
                f.op("dve", V("tensor_tensor", out=ot[:], in0=pd_t[:, 0:256], in1=xt[:], op=ALU.add),
                     reads=[b_xt], writes=[b_ot, b_pd])
                f.dma("sp", V("dma_start", out=y[tsl, pn * 256:(pn + 1) * 256], in_=ot[:]), reads=[b_ot], writes=[b_y])
    f.barrier()
    es4.close()


_IN_SPECS = [
    ("xo", [T, D]), ("xp", [T, D]),
    ("w_in", [D, INW]), ("w_out", [D, D]), ("w_gate", [D, DFF]), ("w_up", [D, DFF]), ("w_down", [DFF, D]),
    ("gcol", [P, 48]), ("bs_col", [P, 8]), ("rowp", [P, 2304]), ("wsT", [P, 8, P]),
    ("ident", [P, P]), ("tril", [P, P]), ("mb", [P, 2, 2, P]), ("cs", [2, P, NT, 2, 64]),
]


def build_layer_nc():
    nc = bass.Bass("TRN2", target_bir_lowering=False)
    io = {}
    for name, shape in _IN_SPECS:
        io[name] = nc.dram_tensor(name, shape, F32, kind="ExternalInput").ap()
    io["y"] = nc.dram_tensor("y", [T, D], F32, kind="ExternalOutput").ap()
    io["qT_d"] = nc.dram_tensor("qT_d", [24, P, T], BF16).ap()
    io["kT_d"] = nc.dram_tensor("kT_d", [8, P, 2 * T], BF16).ap()
    io["vT_d"] = nc.dram_tensor("vT_d", [8, P, 2 * T], BF16).ap()
    io["x1_d"] = nc.dram_tensor("x1_d", [T, D], F32).ap()
    f = FW(nc)
    with ExitStack() as es0:
        emit_layer(nc, f, es0, io, 0)
    f.emit()
    return nc


def _consts():
    ident = np.eye(P, dtype=np.float32)
    k = np.arange(P)[:, None]
    q = np.arange(P)[None, :]
    tril = (k <= q).astype(np.float32)
    prev = np.where(k >= q, 0.0, NEG).astype(np.float32)
    cur = np.where(k <= q, 0.0, NEG).astype(np.float32)
    allneg = np.full((P, P), NEG, np.float32)
    mbB = np.stack([np.stack([prev, cur], 0), np.stack([prev, cur], 0)], 0)
    mbA = np.stack([np.stack([prev, cur], 0), np.stack([allneg, cur], 0)], 0)
    mbA = np.ascontiguousarray(mbA.transpose(2, 0, 1, 3))
    mbB = np.ascontiguousarray(mbB.transpose(2, 0, 1, 3))
    pos = np.arange(2 * T, dtype=np.float32)
    inv_freq = (1.0 / (np.float32(10000.0) ** (np.arange(0, 128, 2, dtype=np.float32) / np.float32(128)))).astype(np.float32)
    ang = pos[:, None] * inv_freq[None, :]
    cos = np.cos(ang).astype(np.float32)
    sin = np.sin(ang).astype(np.float32)
    cst = np.stack([cos, sin], 1)

    def cs_for(start):
        a = cst[start:start + T].reshape(NT, P, 2, 64).transpose(1, 0, 2, 3)
        return np.ascontiguousarray(a)
    csA = np.stack([cs_for(0), cs_for(0)], 0)
    csB = np.stack([cs_for(0), cs_for(T)], 0)
    return ident, tril, (mbA, mbB), (csA, csB)


def _layer_params(l, mix_norm, a_ln_g, a_ln_b, a_w_s, a_b_s, q_norm, k_norm, a_out_norm, b_out_norm, ffn_norm):
    gcol = np.concatenate([
        mix_norm[l].reshape(KC, P).T, ffn_norm[l].reshape(KC, P).T,
        a_out_norm[l].reshape(8, P).T, b_out_norm[l].reshape(8, P).T], axis=1).astype(np.float32)
    bs_col = np.ascontiguousarray(a_b_s[l].T).astype(np.float32)
    row = np.concatenate([a_ln_g[l].reshape(-1), a_ln_b[l].reshape(-1), q_norm[l], k_norm[l]]).astype(np.float32)
    rowp = np.ascontiguousarray(np.broadcast_to(row[None, :], (P, row.size)))
    wsT = np.ascontiguousarray(a_w_s[l].transpose(2, 0, 1)).astype(np.float32)
    return np.ascontiguousarray(gcol), bs_col, rowp, wsT


_NC_CACHE = {}


def kernel(x, mix_norm, w_in, a_ln_g, a_ln_b, a_w_s, a_b_s, q_norm, k_norm,
           a_out_norm, b_out_norm, w_out, ffn_norm, w_gate, w_up, w_down):
    x = np.asarray(x, dtype=np.float32)
    B, S, _ = x.shape
    ident, tril, (mbA, mbB), (csA, csB) = _consts()
    if "nc" not in _NC_CACHE:
        _NC_CACHE["nc"] = build_layer_nc()
    nc = _NC_CACHE["nc"]
    cur = x
    for l in range(2):
        gcol, bs_col, rowp, wsT = _layer_params(l, *[np.asarray(a, np.float32) for a in (
            mix_norm, a_ln_g, a_ln_b, a_w_s, a_b_s, q_norm, k_norm, a_out_norm, b_out_norm, ffn_norm)])
        shared = {
            "w_in": np.ascontiguousarray(w_in[l], dtype=np.float32), "w_out": np.ascontiguousarray(w_out[l], dtype=np.float32),
            "w_gate": np.ascontiguousarray(w_gate[l], dtype=np.float32), "w_up": np.ascontiguousarray(w_up[l], dtype=np.float32),
            "w_down": np.ascontiguousarray(w_down[l], dtype=np.float32),
            "gcol": gcol, "bs_col": bs_col, "rowp": rowp, "wsT": wsT, "ident": ident, "tril": tril,
        }
        in_maps = []
        for core in range(8):
            b, s = core // 2, core % 2
            m = dict(shared)
            m["xo"] = np.ascontiguousarray(cur[b, s * T:(s + 1) * T])
            m["xp"] = np.ascontiguousarray(cur[b, 0:T])
            m["mb"] = mbB if s == 1 else mbA
            m["cs"] = csB if s == 1 else csA
            in_maps.append(m)
        res = run_bass_kernel_spmd(nc, in_maps, core_ids=list(range(8)))
        nxt = np.empty_like(cur)
        for core in range(8):
            b, s = core // 2, core % 2
            nxt[b, s * T:(s + 1) * T] = res.results[core]["y"]
        cur = nxt
    return cur
```
